# Optimizing a Trainium2 kernel written in Bass

```python
import math
import jax, jax.numpy as jnp
from jax import lax
import numpy as np

D_MODEL = 2048
BATCH = 2
SEQ = 8192
DEPTH = 1

D_MIX = D_MODEL
ATTN_WIDTH = D_MIX // 2
SSM_WIDTH = D_MIX - ATTN_WIDTH
DIFF_HEAD_DIM = 64
N_DIFF_HEADS = ATTN_WIDTH // (2 * DIFF_HEAD_DIM)
SSM_GROUP = 16
N_SSM_GROUPS = SSM_WIDTH // SSM_GROUP
SSM_STATE = 64
D_FF = ((8 * D_MODEL) // 3 + 255) // 256 * 256
Q_BLOCK = 128
DT_MIN = 1e-3
DT_MAX = 1e-1
LN_EPS = 1e-5
RMS_EPS = 1e-5
DEEPNORM_ALPHA = (2 * DEPTH) ** 0.25
DEEPNORM_BETA = (8 * DEPTH) ** -0.25
FFN_RES_WEIGHT = 0.5
N_SUBLAYERS = 3
IN_COLS = 3 * ATTN_WIDTH + SSM_WIDTH

kernel_name = 'hymba_style_diffattn_s5_macaron_deepnorm_adaln'


def _layer_norm(h, g, b):
    h32 = h.astype(jnp.float32)
    mu = jnp.mean(h32, -1, keepdims=True)
    var = jnp.mean(jnp.square(h32 - mu), -1, keepdims=True)
    out = (h32 - mu) * lax.rsqrt(var + LN_EPS) * g.astype(jnp.float32) + b.astype(jnp.float32)
    return out.astype(h.dtype)


def _modulate(h, shift, scale):
    return h * (1 + scale[:, None, :]) + shift[:, None, :]


def _swiglu(u, w1, w3, w2):
    return (jax.nn.silu(u @ w1) * (u @ w3)) @ w2


def _diff_attention(q, k, v, lq1, lk1, lq2, lk2, subln_g, lambda_init):
    B, L, _ = q.shape
    H, d = N_DIFF_HEADS, DIFF_HEAD_DIM
    q = q.reshape(B, L, H, 2, d)
    k = k.reshape(B, L, H, 2, d)
    v = v.reshape(B, L, H, 2 * d)
    f32 = jnp.float32
    lam = (jnp.exp(jnp.sum(lq1.astype(f32) * lk1.astype(f32)))
           - jnp.exp(jnp.sum(lq2.astype(f32) * lk2.astype(f32))) + lambda_init)
    scale = 1.0 / math.sqrt(d)
    nb = L // Q_BLOCK
    qb = q.reshape(B, nb, Q_BLOCK, H, 2, d).transpose(1, 0, 2, 3, 4, 5)
    k_pos = jnp.arange(L)

    def block(args):
        q_blk, blk = args
        s = jnp.einsum('bqhcd,bkhcd->bhcqk', q_blk, k).astype(f32) * scale
        q_pos = blk * Q_BLOCK + jnp.arange(Q_BLOCK)
        mask = k_pos[None, :] <= q_pos[:, None]
        s = jnp.where(mask, s, -jnp.inf)
        p = jax.nn.softmax(s, axis=-1)
        w = p[:, :, 0] - lam * p[:, :, 1]
        return jnp.einsum('bhqk,bkhe->bqhe', w.astype(v.dtype), v)

    o = lax.map(block, (qb, jnp.arange(nb)))
    o = o.transpose(1, 0, 2, 3, 4).reshape(B, L, H, 2 * d).astype(f32)
    o = o * lax.rsqrt(jnp.mean(jnp.square(o), -1, keepdims=True) + RMS_EPS)
    o = o * subln_g.astype(f32) * (1.0 - lambda_init)
    return o.reshape(B, L, ATTN_WIDTH).astype(q.dtype)


def _s5_branch(s, a_re, a_im, log_dt, b_re, b_im, c_re, c_im, d_skip, glu_w, glu_b):
    B, L, _ = s.shape
    G, P, Hc = N_SSM_GROUPS, SSM_STATE, SSM_GROUP
    f32 = jnp.float32
    a_re = a_re.astype(f32); a_im = a_im.astype(f32)
    b_re = b_re.astype(f32); b_im = b_im.astype(f32)
    dt = jnp.exp(log_dt.astype(f32))[:, None]
    mag = jnp.exp(a_re * dt)
    lbar_re = mag * jnp.cos(a_im * dt)
    lbar_im = mag * jnp.sin(a_im * dt)
    nr = lbar_re - 1.0
    ni = lbar_im
    den = jnp.square(a_re) + jnp.square(a_im)
    coef_re = ((nr * a_re + ni * a_im) / den)[..., None]
    coef_im = ((ni * a_re - nr * a_im) / den)[..., None]
    bb_re = coef_re * b_re - coef_im * b_im
    bb_im = coef_re * b_im + coef_im * b_re
    u = s.astype(f32).reshape(B, L, G, Hc)
    x_re = jnp.einsum('blgh,gph->lbgp', u, bb_re)
    x_im = jnp.einsum('blgh,gph->lbgp', u, bb_im)
    al_re = jnp.broadcast_to(lbar_re[None, None], (L, 1, G, P))
    al_im = jnp.broadcast_to(lbar_im[None, None], (L, 1, G, P))

    def combine(e1, e2):
        a1r, a1i, b1r, b1i = e1
        a2r, a2i, b2r, b2i = e2
        return (a2r * a1r - a2i * a1i,
                a2r * a1i + a2i * a1r,
                a2r * b1r - a2i * b1i + b2r,
                a2r * b1i + a2i * b1r + b2i)

    _, _, h_re, h_im = lax.associative_scan(combine, (al_re, al_im, x_re, x_im), axis=0)
    y = (jnp.einsum('lbgp,ghp->blgh', h_re, c_re.astype(f32))
         - jnp.einsum('lbgp,ghp->blgh', h_im, c_im.astype(f32)))
    y = y.reshape(B, L, SSM_WIDTH) + d_skip.astype(f32) * u.reshape(B, L, SSM_WIDTH)
    g = jax.nn.gelu(y)
    out = g * jax.nn.sigmoid(g @ glu_w.astype(f32) + glu_b.astype(f32))
    return out.astype(s.dtype)


def _hybrid_mixer(u, w_in, lq1, lk1, lq2, lk2, subln_g, a_re, a_im, log_dt, b_re, b_im,
                  c_re, c_im, d_skip, glu_w, glu_b, w_out, lambda_init):
    proj = u @ w_in
    q, k, v, s = jnp.split(proj, [ATTN_WIDTH, 2 * ATTN_WIDTH, 3 * ATTN_WIDTH], axis=-1)
    attn = _diff_attention(q, k, v, lq1, lk1, lq2, lk2, subln_g, lambda_init)
    ssm = _s5_branch(s, a_re, a_im, log_dt, b_re, b_im, c_re, c_im, d_skip, glu_w, glu_b)
    return jnp.concatenate([attn, ssm], axis=-1) @ w_out


def setup_inputs(seed: int = 0) -> dict:
    key = jax.random.key(seed)
    ks = jax.random.split(key, 40)
    f32 = jnp.float32

    def nrm(k, shape, s):
        return jax.random.normal(k, shape, f32) * s

    D, F, G, P, Hc = D_MODEL, D_FF, N_SSM_GROUPS, SSM_STATE, SSM_GROUP
    n_idx = jnp.arange(P, dtype=f32)
    w_in = jnp.concatenate([
        nrm(ks[8], (DEPTH, D, 2 * ATTN_WIDTH), D ** -0.5),
        nrm(ks[9], (DEPTH, D, ATTN_WIDTH), D ** -0.5) * DEEPNORM_BETA,
        nrm(ks[10], (DEPTH, D, SSM_WIDTH), D ** -0.5)], axis=-1)
    return {
        'x': nrm(ks[0], (BATCH, SEQ, D), 1.0),
        'c': nrm(ks[1], (BATCH, D), 1.0),
        'w_cond': nrm(ks[2], (DEPTH, D, N_SUBLAYERS * 3 * D), 0.5 * D ** -0.5),
        'b_cond': nrm(ks[3], (DEPTH, N_SUBLAYERS * 3 * D), 0.01),
        'ffn1_w1': nrm(ks[4], (DEPTH, D, F), D ** -0.5),
        'ffn1_w3': nrm(ks[5], (DEPTH, D, F), D ** -0.5),
        'ffn1_w2': nrm(ks[6], (DEPTH, F, D), F ** -0.5) * DEEPNORM_BETA,
        'w_in': w_in,
        'lambda_q1': nrm(ks[11], (DEPTH, DIFF_HEAD_DIM), 0.1),
        'lambda_k1': nrm(ks[12], (DEPTH, DIFF_HEAD_DIM), 0.1),
        'lambda_q2': nrm(ks[13], (DEPTH, DIFF_HEAD_DIM), 0.1),
        'lambda_k2': nrm(ks[14], (DEPTH, DIFF_HEAD_DIM), 0.1),
        'subln_g': 1.0 + nrm(ks[15], (DEPTH, 2 * DIFF_HEAD_DIM), 0.01),
        'ssm_a_re': -0.5 + nrm(ks[16], (DEPTH, G, P), 0.01),
        'ssm_a_im': math.pi * n_idx + nrm(ks[17], (DEPTH, G, P), 0.01),
        'ssm_log_dt': jax.random.uniform(ks[18], (DEPTH, G), f32, math.log(DT_MIN), math.log(DT_MAX)),
        'ssm_b_re': nrm(ks[19], (DEPTH, G, P, Hc), (2 * Hc) ** -0.5),
        'ssm_b_im': nrm(ks[20], (DEPTH, G, P, Hc), (2 * Hc) ** -0.5),
        'ssm_c_re': nrm(ks[21], (DEPTH, G, Hc, P), (2 * P) ** -0.5),
        'ssm_c_im': nrm(ks[22], (DEPTH, G, Hc, P), (2 * P) ** -0.5),
        'ssm_d': nrm(ks[23], (DEPTH, SSM_WIDTH), 1.0),
        'glu_w': nrm(ks[24], (DEPTH, SSM_WIDTH, SSM_WIDTH), SSM_WIDTH ** -0.5),
        'glu_b': nrm(ks[25], (DEPTH, SSM_WIDTH), 0.01),
        'w_out': nrm(ks[26], (DEPTH, D_MIX, D), D_MIX ** -0.5) * DEEPNORM_BETA,
        'ffn2_w1': nrm(ks[27], (DEPTH, D, F), D ** -0.5),
        'ffn2_w3': nrm(ks[28], (DEPTH, D, F), D ** -0.5),
        'ffn2_w2': nrm(ks[29], (DEPTH, F, D), F ** -0.5) * DEEPNORM_BETA,
        'ln_g': 1.0 + nrm(ks[30], (DEPTH, N_SUBLAYERS, D), 0.01),
        'ln_b': nrm(ks[31], (DEPTH, N_SUBLAYERS, D), 0.01),
    }


def reference(x, c, w_cond, b_cond, ffn1_w1, ffn1_w3, ffn1_w2, w_in, lambda_q1, lambda_k1,
              lambda_q2, lambda_k2, subln_g, ssm_a_re, ssm_a_im, ssm_log_dt, ssm_b_re, ssm_b_im,
              ssm_c_re, ssm_c_im, ssm_d, glu_w, glu_b, w_out, ffn2_w1, ffn2_w3, ffn2_w2, ln_g, ln_b):
    B = x.shape[0]
    for i in range(DEPTH):
        lambda_init = 0.8 - 0.6 * math.exp(-0.3 * i)
        mod = (jax.nn.silu(c) @ w_cond[i] + b_cond[i]).reshape(B, N_SUBLAYERS, 3, D_MODEL).astype(x.dtype)

        u = _modulate(x, mod[:, 0, 0], mod[:, 0, 1])
        h = _swiglu(u, ffn1_w1[i], ffn1_w3[i], ffn1_w2[i])
        x = _layer_norm(DEEPNORM_ALPHA * x + FFN_RES_WEIGHT * (1 + mod[:, 0, 2])[:, None, :] * h,
                        ln_g[i, 0], ln_b[i, 0])

        u = _modulate(x, mod[:, 1, 0], mod[:, 1, 1])
        h = _hybrid_mixer(u, w_in[i], lambda_q1[i], lambda_k1[i], lambda_q2[i], lambda_k2[i], subln_g[i],
                          ssm_a_re[i], ssm_a_im[i], ssm_log_dt[i], ssm_b_re[i], ssm_b_im[i],
                          ssm_c_re[i], ssm_c_im[i], ssm_d[i], glu_w[i], glu_b[i], w_out[i], lambda_init)
        x = _layer_norm(DEEPNORM_ALPHA * x + (1 + mod[:, 1, 2])[:, None, :] * h, ln_g[i, 1], ln_b[i, 1])

        u = _modulate(x, mod[:, 2, 0], mod[:, 2, 1])
        h = _swiglu(u, ffn2_w1[i], ffn2_w3[i], ffn2_w2[i])
        x = _layer_norm(DEEPNORM_ALPHA * x + FFN_RES_WEIGHT * (1 + mod[:, 2, 2])[:, None, :] * h,
                        ln_g[i, 2], ln_b[i, 2])
    return x
```

```python
import math
import numpy as np
from contextlib import ExitStack
import concourse.bass as bass
import concourse.mybir as mybir
from concourse.bass_utils import run_bass_kernel_spmd

F32 = mybir.dt.float32
BF16 = mybir.dt.bfloat16
I32 = mybir.dt.int32
AF = mybir.ActivationFunctionType
ALU = mybir.AluOpType

D = 2048
FF = 5632
FC = FF // 128
KC = D // 128
NTOK = 2048
TT = 512
NTT = NTOK // TT
SEQ = 8192
ALPHA = 2.0 ** 0.25
LN_EPS = 1e-5
RMS_EPS = 1e-5
LAMBDA_INIT = 0.2
DEBUG_STOP = None


class DSem:
    def __init__(self, nc, name):
        self.sem = nc.alloc_semaphore(name)
        self.n = 0


class Prog:
    ENG = ("pe", "act", "dve", "pool", "sp")

    def __init__(self, nc):
        self.nc = nc
        self.csem = {e: nc.alloc_semaphore("cs_" + e) for e in self.ENG}
        self.cval = {e: 0 for e in self.ENG}
        self.reset_phase()

    def reset_phase(self):
        self.ops = []
        self.last_w = {}
        self.readers = {}
        self.eng_idx = {e: 0 for e in self.ENG}
        self.dma_tokens = []

    def _deps(self, reads, writes):
        deps = []
        for k in reads:
            if k in self.last_w:
                deps.append((self.last_w[k], True))
        for k in writes:
            if k in self.last_w:
                deps.append((self.last_w[k], False))
            for t in self.readers.get(k, ()):
                deps.append((t, False))
        return deps

    def _record(self, tok, reads, writes):
        for k in reads:
            self.readers.setdefault(k, []).append(tok)
        for k in writes:
            self.last_w[k] = tok
            self.readers[k] = []

    def op(self, eng, fn, reads=(), writes=()):
        deps = self._deps(reads, writes)
        rec = {"eng": eng, "fn": fn, "deps": deps, "kind": "c", "idx": self.eng_idx[eng],
               "signal": False, "id": len(self.ops)}
        self.eng_idx[eng] += 1
        self.ops.append(rec)
        self._record(rec, reads, writes)
        return rec

    def dma(self, eng, fn, dsem, reads=(), writes=()):
        deps = self._deps(reads, writes)
        dsem.n += 16
        rec = {"eng": eng, "fn": fn, "deps": deps, "kind": "d", "idx": self.eng_idx[eng],
               "dsem": dsem, "dval": dsem.n, "id": len(self.ops)}
        self.eng_idx[eng] += 1
        self.ops.append(rec)
        self._record(rec, reads, writes)
        self.dma_tokens.append(rec)
        return rec

    def cc(self, fn, reads=(), writes=()):
        deps = self._deps(reads, writes)
        ds = DSem(self.nc, f"ccs_{len(self.ops)}_{self.cval['pe']}_{self.cval['dve']}")
        ds.n += 1
        rec = {"eng": "pool", "fn": fn, "deps": deps, "kind": "d", "idx": self.eng_idx["pool"],
               "dsem": ds, "dval": ds.n, "id": len(self.ops), "cc": True}
        self.eng_idx["pool"] += 1
        self.ops.append(rec)
        self._record(rec, reads, writes)
        self.dma_tokens.append(rec)
        return rec

    def flush(self):
        nc = self.nc
        fin = {"eng": "sp", "fn": None, "deps": [(t, True) for t in self.dma_tokens],
               "kind": "c", "idx": self.eng_idx["sp"], "signal": False, "id": len(self.ops)}
        self.ops.append(fin)
        for o in self.ops:
            need = []
            for (p, raw) in o["deps"]:
                if p["kind"] == "d":
                    need.append(p)
                    continue
                if p["eng"] == o["eng"]:
                    if raw and (o["idx"] - p["idx"]) <= 2 and o["eng"] != "pe":
                        p["signal"] = True
                        need.append(p)
                else:
                    p["signal"] = True
                    need.append(p)
            o["need"] = need
        for e in self.ENG:
            for o in self.ops:
                if o["eng"] == e and o["kind"] == "c" and o["signal"]:
                    self.cval[e] += 1
                    o["sval"] = self.cval[e]
        ops = self.ops
        csem = self.csem

        def emit(engname, eng):
            waited = {}
            for o in ops:
                if o["eng"] != engname:
                    continue
                for p in o["need"]:
                    if p["kind"] == "d":
                        key, val, sem = ("d", id(p["dsem"])), p["dval"], p["dsem"].sem
                    else:
                        key, val, sem = ("c", p["eng"]), p["sval"], csem[p["eng"]]
                    if waited.get(key, 0) >= val:
                        continue
                    waited[key] = val
                    eng.wait_ge(sem, val)
                if o["fn"] is None:
                    continue
                ins = o["fn"](eng)
                if o.get("cc"):
                    ins.then_inc(o["dsem"].sem)
                elif o["kind"] == "d":
                    ins.then_inc(o["dsem"].sem, 16)
                elif o["signal"]:
                    ins.then_inc(csem[engname], 1)

        with nc.Block() as blk:
            @blk.tensor
            def _(e):
                emit("pe", e)

            @blk.scalar
            def _(e):
                emit("act", e)

            @blk.vector
            def _(e):
                emit("dve", e)

            @blk.gpsimd
            def _(e):
                emit("pool", e)

            @blk.sync
            def _(e):
                emit("sp", e)
        self.reset_phase()


def build_program(stop=None):
    nc = bass.Bass("TRN2", target_bir_lowering=False)

    def din(name, shape, dt=F32):
        return nc.dram_tensor(name, list(shape), dt, kind="ExternalInput").ap()

    x_in = din("x", [SEQ, D])
    c_col = din("c_col", [128, KC])
    w_cond = din("w_cond", [D, 9 * D])
    b_cond = din("b_cond", [1, 9 * D])
    ffw = {}
    for s in (1, 2):
        ffw[s] = (din(f"ffn{s}_w1", [D, FF]), din(f"ffn{s}_w3", [D, FF]), din(f"ffn{s}_w2", [FF, D]))
    ln_g = din("ln_g", [3, D])
    ln_b = din("ln_b", [3, D])
    ident_in = din("ident", [128, 128])
    w_in = din("w_in", [D, 4096])
    ssm_cols_in = din("ssm_cols", [4] + [128, 8, 3])
    ssm_rep_in = din("ssm_rep", [4] + [128, 2, 3, 64])
    bT_in = din("bT", [4] + [128, 2, 2, 64])
    cT_in = din("cT", [4] + [128, 8, 2, 16])
    dcol_in = din("dcol", [4] + [128, 2])
    mask_g8_in = din("mask_g8", [128, 8])
    mask_gi_in = din("mask_gi", [128, 2])
    lam4_in = din("lam4", [64, 4])
    subg_in = din("subg", [128, 1])
    tri_in = din("tri", [128, 128])
    glu_w = din("glu_w", [1024, 1024])
    glu_bT = din("glu_bT", [128, 8])
    w_out = din("w_out", [D, D])
    meta = din("meta", [1, 8], I32)
    S1 = nc.dram_tensor("S1", [4096, 2048], BF16).ap()
    G1 = nc.dram_tensor("G1", [8 * 4096, 2048], BF16).ap()
    S2 = nc.dram_tensor("S2", [512, SEQ], BF16).ap()
    G2 = nc.dram_tensor("G2", [8 * 512, SEQ], BF16).ap()
    L1 = nc.dram_tensor("L1", [4, 4, 1, 256, 2048], BF16).ap()
    L2 = nc.dram_tensor("L2", [4, 512, 1, 2048], BF16).ap()
    G1v = G1.rearrange("(rk sec blk p) c -> rk sec blk p c", rk=8, sec=4, blk=4, p=256)
    G2v = G2.rearrange("(rk p) (blk c) -> rk p blk c", rk=8, blk=4)
    out_d = nc.dram_tensor("out", [SEQ, D], F32, kind="ExternalOutput").ap()

    gvec_d = nc.dram_tensor("gvec_d", [3, 128, D], F32).ap()
    x1_full = nc.dram_tensor("x1_full", [SEQ, D], F32).ap()
    x1_d3 = nc.dram_tensor("x1_d", [1, NTOK, D], F32).ap()
    LF = nc.dram_tensor("LF", [4, 4096, 2048], BF16).ap()
    LF5 = LF.rearrange("rk (sec blk p) c -> rk sec blk p c", sec=4, blk=4, p=256)
    L2F = nc.dram_tensor("L2F", [4, 512, SEQ], BF16).ap()
    L2Fv = L2F.rearrange("k p (blk c) -> k p blk c", blk=4)
    x2_d = nc.dram_tensor("x2_d", [SEQ, D], F32).ap()

    P = Prog(nc)
    ds_cnt = [0]

    free_ds, live_ds = {}, []

    def dsem(name, q="sp"):
        fl = free_ds.setdefault(name, [])
        if fl:
            d_ = fl.pop()
        else:
            ds_cnt[0] += 1
            d_ = DSem(nc, f"ds_{name}_{ds_cnt[0]}")
        live_ds.append((name, d_))
        return d_

    P.mk_dsem = dsem
    _flush0 = P.flush

    def _flush1():
        _flush0()
        for nm_, d_ in live_ds:
            free_ds.setdefault(nm_, []).append(d_)
        del live_ds[:]

    P.flush = _flush1

    with ExitStack() as glob:
        def gsb(name, shape, dt):
            return glob.enter_context(nc.sbuf_tensor("g_" + name, list(shape), dt))

        ident = gsb("ident", [128, 128], F32)
        modcol = gsb("modcol", [128, 6, KC], F32)

        with ExitStack() as es:
            def sb(name, shape, dt):
                return es.enter_context(nc.sbuf_tensor("s_" + name, list(shape), dt))

            def ps(name):
                return es.enter_context(nc.psum_tensor("s_" + name, [128, 512], F32))

            ccol = sb("ccol", [128, KC], F32)
            sc = sb("sc", [128, KC], F32)
            sc_rep = sb("sc_rep", [128, KC, 128], BF16)
            wc = [sb(f"wc{i}", [128, KC, 512], BF16) for i in range(2)]
            wc_s = [dsem("wc", "pool") for _ in range(2)]
            brow = sb("brow", [128, D], F32)
            row = sb("row", [128, D], F32)
            psm = [ps(f"psm{i}") for i in range(2)]
            pst = [ps(f"pst{i}") for i in range(2)]
            s_misc = dsem("misc")
            s_brow = dsem("brow")
            s_gv = dsem("gv")

            P.dma("sp", lambda e: e.dma_start(out=ident[:], in_=ident_in), dsem("ident"), writes=["ident"])
            P.dma("sp", lambda e: e.dma_start(out=ccol[:], in_=c_col), s_misc, writes=["ccol"])
            P.op("act", lambda e: e.activation(out=sc[:], in_=ccol[:], func=AF.Silu),
                 reads=["ccol"], writes=["sc"])
            P.op("dve", lambda e: e.tensor_copy(out=sc_rep[:], in_=sc[:].unsqueeze(2).broadcast_to([128, KC, 128])),
                 reads=["sc"], writes=["sc_rep"])
            wv = w_cond.rearrange("(kc p) f -> p kc f", p=128)
            blk_i = 0
            for j in range(9):
                s, kind = j // 3, j % 3
                P.dma("sp", lambda e, j=j: e.dma_start(out=brow[:], in_=b_cond[:, j * D:(j + 1) * D].partition_broadcast(128)),
                      s_brow, writes=["brow"])
                for cb in range(4):
                    slot = blk_i % 2
                    col0 = j * D + cb * 512
                    P.dma("pool", lambda e, slot=slot, col0=col0: e.dma_start(out=wc[slot][:], in_=wv[:, :, col0:col0 + 512]),
                          wc_s[slot], writes=[("wc", slot)])

                    def mm(e, slot=slot):
                        for kc in range(KC):
                            i = e.matmul(psm[slot][:], lhsT=sc_rep[:, kc, :], rhs=wc[slot][:, kc, :],
                                         start=(kc == 0), stop=(kc == KC - 1))
                        return i
                    P.op("pe", mm, reads=["sc_rep", ("wc", slot)], writes=[("psm", slot)])
                    P.op("dve", lambda e, slot=slot, cb=cb: e.tensor_tensor(
                        out=row[:, cb * 512:(cb + 1) * 512], in0=psm[slot][:], in1=brow[:, cb * 512:(cb + 1) * 512], op=ALU.add),
                        reads=[("psm", slot), "brow"], writes=["row"])
                    blk_i += 1
                if kind == 2:
                    wres = 1.0 if s == 1 else 0.5
                    P.op("dve", lambda e, wres=wres: e.tensor_scalar(out=row[:], in0=row[:], scalar1=1.0, scalar2=wres,
                                                                     op0=ALU.add, op1=ALU.mult),
                         reads=["row"], writes=["row"])
                    P.dma("sp", lambda e, s=s: e.dma_start(out=gvec_d[s], in_=row[:]), s_gv,
                          reads=["row"], writes=[("gvec_d", s)])
                else:
                    if kind == 1:
                        P.op("dve", lambda e: e.tensor_scalar(out=row[:], in0=row[:], scalar1=1.0, scalar2=None, op0=ALU.add),
                             reads=["row"], writes=["row"])
                    for q in range(4):
                        pb = q % 2

                        def tr(e, q=q, pb=pb):
                            for a in range(4):
                                fc = q * 4 + a
                                i = e.transpose(out=pst[pb][:, a * 128:(a + 1) * 128], in_=row[:, fc * 128:(fc + 1) * 128],
                                                identity=ident[:])
                            return i
                        P.op("pe", tr, reads=["row", "ident"], writes=[("pst", pb)])
                        P.op("dve", lambda e, q=q, pb=pb, s=s, kind=kind: e.tensor_copy(
                            out=modcol[:, 2 * s + kind, q * 4:(q + 1) * 4],
                            in_=pst[pb][:].rearrange("p (a b) -> p a b", b=128)[:, :, 0]),
                            reads=[("pst", pb)], writes=["modcol"])
            P.flush()
        if stop == "setup":
            return nc, P

        def ffn_stage(s, x_src, dst, w1, w3, w2, tag, ntt=NTT):
            with ExitStack() as es:
                def sb(name, shape, dt):
                    return es.enter_context(nc.sbuf_tensor(f"{tag}_{name}", list(shape), dt))

                def ps(name):
                    return es.enter_context(nc.psum_tensor(f"{tag}_{name}", [128, 512], F32))

                uT = sb("uT", [128, KC, TT], BF16)
                gT = sb("gT", [128, FC, TT], BF16)
                NA = 6
                wA = [sb(f"wA{i}", [128, KC, 128], BF16) for i in range(NA)]
                wA_s = [dsem("wA", "pool") for _ in range(NA)]
                wB = [sb(f"wB{i}", [128, FC, 256], BF16) for i in range(2)]
                wB_s = [dsem("wB", "pool") for _ in range(2)]
                pre = sb("pre", [128, 4, D], F32)
                xs = [sb(f"xs{i}", [128, D], F32) for i in range(2)]
                xs_s = [dsem("xs") for _ in range(2)]
                gvec = sb("gvec", [128, D], F32)
                lng = sb("lng", [128, D], F32)
                lnb = sb("lnb", [128, D], F32)
                sil = [sb(f"sil{i}", [128, TT], BF16) for i in range(2)]
                stats = sb("stats", [128, 4, 6], F32)
                mv = sb("mv", [128, 2], F32)
                rstd = sb("rstd", [128, 1], F32)
                nmr = sb("nmr", [128, 1], F32)
                psT = [ps(f"psT{i}") for i in range(2)]
                psH1 = [ps(f"psH1{i}") for i in range(2)]
                psH3 = [ps(f"psH3{i}") for i in range(2)]
                psO = [ps(f"psO{i}") for i in range(2)]
                s_rows = dsem("rows")
                s_st = [dsem("st") for _ in range(4)]

                P.dma("sp", lambda e: e.dma_start(out=gvec[:], in_=gvec_d[s]), dsem("gvec"), writes=["gvec"])
                P.dma("sp", lambda e: e.dma_start(out=lng[:], in_=ln_g[s:s + 1, :].partition_broadcast(128)), dsem("lng"), writes=["lng"])
                P.dma("sp", lambda e: e.dma_start(out=lnb[:], in_=ln_b[s:s + 1, :].partition_broadcast(128)), dsem("lnb"), writes=["lnb"])
                w1v = w1.rearrange("(kc p) f -> p kc f", p=128)
                w3v = w3.rearrange("(kc p) f -> p kc f", p=128)
                w2v = w2.rearrange("(fc p) d -> p fc d", p=128)
                na = 0
                nb = 0
                xi = 0
                for t in range(ntt):
                    for sub in range(4):
                        r0 = t * TT + sub * 128
                        xb = xi % 2
                        xi += 1
                        P.dma("sp", lambda e, xb=xb, r0=r0: e.dma_start(out=xs[xb][:], in_=x_src[r0:r0 + 128, :]),
                              xs_s[xb], writes=[("xs", xb)])
                        for q in range(4):
                            pb = q % 2

                            def tr(e, q=q, pb=pb, xb=xb):
                                for a in range(4):
                                    fc = q * 4 + a
                                    i = e.transpose(out=psT[pb][:, a * 128:(a + 1) * 128],
                                                    in_=xs[xb][:, fc * 128:(fc + 1) * 128], identity=ident[:])
                                return i
                            P.op("pe", tr, reads=[("xs", xb), "ident"], writes=[("psT", pb)])

                            def ev(e, q=q, pb=pb, sub=sub):
                                for a in range(4):
                                    fc = q * 4 + a
                                    i = e.activation(out=uT[:, fc, sub * 128:(sub + 1) * 128], in_=psT[pb][:, a * 128:(a + 1) * 128],
                                                     func=AF.Identity, bias=modcol[:, 2 * s, fc:fc + 1],
                                                     scale=modcol[:, 2 * s + 1, fc:fc + 1])
                                return i
                            P.op("act", ev, reads=[("psT", pb), "modcol"], writes=["uT"])
                    for fc in range(FC):
                        par = fc % 2
                        sl1 = na % NA
                        na += 1
                        sl3 = na % NA
                        na += 1
                        P.dma("pool", lambda e, sl=sl1, fc=fc: e.dma_start(out=wA[sl][:], in_=w1v[:, :, fc * 128:(fc + 1) * 128]),
                              wA_s[sl1], writes=[("wA", sl1)])
                        P.dma("pool", lambda e, sl=sl3, fc=fc: e.dma_start(out=wA[sl][:], in_=w3v[:, :, fc * 128:(fc + 1) * 128]),
                              wA_s[sl3], writes=[("wA", sl3)])

                        def mmh(e, sl, dstp):
                            for kc in range(KC):
                                i = e.matmul(dstp[:], lhsT=wA[sl][:, kc, :], rhs=uT[:, kc, :], start=(kc == 0), stop=(kc == KC - 1))
                            return i
                        P.op("pe", lambda e, sl=sl1, par=par: mmh(e, sl, psH1[par]), reads=[("wA", sl1), "uT"], writes=[("psH1", par)])
                        P.op("pe", lambda e, sl=sl3, par=par: mmh(e, sl, psH3[par]), reads=[("wA", sl3), "uT"], writes=[("psH3", par)])
                        P.op("act", lambda e, par=par: e.activation(out=sil[par][:], in_=psH1[par][:], func=AF.Silu),
                             reads=[("psH1", par)], writes=[("sil", par)])
                        P.op("dve", lambda e, par=par, fc=fc: e.tensor_tensor(out=gT[:, fc, :], in0=sil[par][:], in1=psH3[par][:], op=ALU.mult),
                             reads=[("sil", par), ("psH3", par)], writes=["gT"])
                    for dq in range(8):
                        slb = nb % 2
                        nb += 1
                        P.dma("pool", lambda e, slb=slb, dq=dq: e.dma_start(out=wB[slb][:], in_=w2v[:, :, dq * 256:(dq + 1) * 256]),
                              wB_s[slb], writes=[("wB", slb)])
                        for sub in range(4):
                            po = (dq * 4 + sub) % 2

                            def mmo(e, slb=slb, sub=sub, po=po):
                                for fc in range(FC):
                                    i = e.matmul(psO[po][:, 0:256], lhsT=gT[:, fc, sub * 128:(sub + 1) * 128], rhs=wB[slb][:, fc, :],
                                                 start=(fc == 0), stop=(fc == FC - 1))
                                return i
                            P.op("pe", mmo, reads=["gT", ("wB", slb)], writes=[("psO", po)])
                            P.op("dve", lambda e, sub=sub, dq=dq, po=po: e.tensor_tensor(
                                out=pre[:, sub, dq * 256:(dq + 1) * 256], in0=psO[po][:, 0:256], in1=gvec[:, dq * 256:(dq + 1) * 256], op=ALU.mult),
                                reads=[("psO", po), "gvec"], writes=[("pre", sub)])
                    for sub in range(4):
                        r0 = t * TT + sub * 128
                        xb = xi % 2
                        xi += 1
                        P.dma("sp", lambda e, xb=xb, r0=r0: e.dma_start(out=xs[xb][:], in_=x_src[r0:r0 + 128, :]),
                              xs_s[xb], writes=[("xs", xb)])
                        pk = ("pre", sub)
                        P.op("dve", lambda e, xb=xb, sub=sub: e.scalar_tensor_tensor(
                            out=pre[:, sub, :], in0=xs[xb][:], scalar=ALPHA, in1=pre[:, sub, :], op0=ALU.mult, op1=ALU.add),
                            reads=[("xs", xb), pk], writes=[pk])

                        def bst(e, sub=sub):
                            for a in range(4):
                                i = e.bn_stats(out=stats[:, a, :], in_=pre[:, sub, a * 512:(a + 1) * 512])
                            return i
                        P.op("dve", bst, reads=[pk], writes=["stats"])
                        P.op("dve", lambda e: e.bn_aggr(out=mv[:], in_=stats[:].rearrange("p a b -> p (a b)")), reads=["stats"], writes=["mv"])
                        P.op("dve", lambda e: e.tensor_scalar(out=rstd[:], in0=mv[:, 1:2], scalar1=LN_EPS, scalar2=None,
                                                              op0=ALU.add), reads=["mv"], writes=["rstd"])
                        P.op("act", lambda e: e.sqrt(out=rstd[:], in_=rstd[:]), reads=["rstd"], writes=["rstd"])
                        P.op("dve", lambda e: e.reciprocal(out=rstd[:], in_=rstd[:]), reads=["rstd"], writes=["rstd"])
                        P.op("dve", lambda e: e.scalar_tensor_tensor(out=nmr[:], in0=mv[:, 0:1], scalar=-1.0, in1=rstd[:],
                                                                     op0=ALU.mult, op1=ALU.mult), reads=["mv", "rstd"], writes=["nmr"])
                        P.op("dve", lambda e, sub=sub: e.tensor_scalar(out=pre[:, sub, :], in0=pre[:, sub, :], scalar1=rstd[:], scalar2=nmr[:],
                                                                       op0=ALU.mult, op1=ALU.add), reads=[pk, "rstd", "nmr"], writes=[pk])
                        P.op("dve", lambda e, sub=sub: e.tensor_tensor(out=pre[:, sub, :], in0=pre[:, sub, :], in1=lng[:], op=ALU.mult),
                             reads=[pk, "lng"], writes=[pk])
                        P.op("dve", lambda e, sub=sub: e.tensor_tensor(out=pre[:, sub, :], in0=pre[:, sub, :], in1=lnb[:], op=ALU.add),
                             reads=[pk, "lnb"], writes=[pk])
                        P.dma("sp", lambda e, sub=sub, r0=r0: e.dma_start(out=dst[r0:r0 + 128, :], in_=pre[:, sub, :]), s_st[sub],
                              reads=[pk], writes=[(tag + "_dst", r0)])
                P.flush()

        def proj_stage():
            tag = "pj"
            with ExitStack() as es:
                def sb(name, shape, dt):
                    return es.enter_context(nc.sbuf_tensor(f"{tag}_{name}", list(shape), dt))

                def ps(name):
                    return es.enter_context(nc.psum_tensor(f"{tag}_{name}", [128, 512], F32))

                uT = sb("uT", [128, KC, TT], BF16)
                NA = 6
                wA = [sb(f"wA{i}", [128, KC, 128], BF16) for i in range(NA)]
                wA_s = [dsem("pwA", "pool") for _ in range(NA)]
                xs = [sb(f"xs{i}", [128, D], F32) for i in range(2)]
                xs_s = [dsem("pxs") for _ in range(2)]
                stg = [sb(f"stg{i}", [128, TT], BF16) for i in range(2)]
                stg_s = [dsem("pstg") for _ in range(2)]
                vstg = [sb(f"vstg{i}", [128, 256], BF16) for i in range(2)]
                vstg_s = [dsem("pvstg") for _ in range(2)]
                psT = [ps(f"psT{i}") for i in range(2)]
                psA = [ps(f"psA{i}") for i in range(2)]
                psV = [ps(f"psV{i}") for i in range(2)]
                wv = w_in.rearrange("(kc p) f -> p kc f", p=128)
                S1vs = [LF[k_, 3072:4096, :].rearrange("(h a) (b e) -> h (a b) e", h=4, b=8, e=256) for k_ in range(4)]
                na = 0
                xi = 0
                ei = 0
                vi = 0
                for t in range(16):
                    for sub in range(4):
                        r0 = t * TT + sub * 128
                        xb = xi % 2
                        xi += 1
                        P.dma("sp", lambda e, xb=xb, r0=r0: e.dma_start(out=xs[xb][:], in_=x1_full[r0:r0 + 128, :]),
                              xs_s[xb], reads=[("f1_dst", r0)], writes=[("xs", xb)])
                        for q in range(4):
                            pb = q % 2

                            def tr(e, q=q, pb=pb, xb=xb):
                                for a in range(4):
                                    fc = q * 4 + a
                                    i = e.transpose(out=psT[pb][:, a * 128:(a + 1) * 128],
                                                    in_=xs[xb][:, fc * 128:(fc + 1) * 128], identity=ident[:])
                                return i
                            P.op("pe", tr, reads=[("xs", xb), "ident"], writes=[("psT", pb)])

                            def ev(e, q=q, pb=pb, sub=sub):
                                for a in range(4):
                                    fc = q * 4 + a
                                    i = e.activation(out=uT[:, fc, sub * 128:(sub + 1) * 128], in_=psT[pb][:, a * 128:(a + 1) * 128],
                                                     func=AF.Identity, bias=modcol[:, 2, fc:fc + 1], scale=modcol[:, 3, fc:fc + 1])
                                return i
                            P.op("act", ev, reads=[("psT", pb), "modcol"], writes=["uT"])
                    for cc in range(24):
                        col0 = cc * 128 if cc < 16 else 3072 + (cc - 16) * 128
                        row0 = cc * 128
                        sl = na % NA
                        na += 1
                        par = ei % 2
                        ei += 1
                        P.dma("pool", lambda e, sl=sl, col0=col0: e.dma_start(out=wA[sl][:], in_=wv[:, :, col0:col0 + 128]),
                              wA_s[sl], writes=[("wA", sl)])

                        def mm(e, sl=sl, par=par):
                            for kc in range(KC):
                                i = e.matmul(psA[par][:], lhsT=wA[sl][:, kc, :], rhs=uT[:, kc, :], start=(kc == 0), stop=(kc == KC - 1))
                            return i
                        P.op("pe", mm, reads=[("wA", sl), "uT"], writes=[("psA", par)])
                        P.op("act", lambda e, par=par: e.activation(out=stg[par][:], in_=psA[par][:], func=AF.Identity),
                             reads=[("psA", par)], writes=[("stg", par)])
                        P.dma("sp", lambda e, par=par, row0=row0, t=t: e.dma_start(out=LF[t // 4, row0:row0 + 128, (t % 4) * TT:(t % 4 + 1) * TT], in_=stg[par][:]),
                              stg_s[par], reads=[("stg", par)], writes=["S1"])
                    for vp in range(4):
                        sls = []
                        for half in range(2):
                            col0 = 2048 + (2 * vp + half) * 128
                            sl = na % NA
                            na += 1
                            sls.append(sl)
                            P.dma("pool", lambda e, sl=sl, col0=col0: e.dma_start(out=wA[sl][:], in_=wv[:, :, col0:col0 + 128]),
                                  wA_s[sl], writes=[("wA", sl)])
                        for sub in range(4):
                            pv = vi % 2
                            vi += 1

                            def mmv(e, sls=tuple(sls), pv=pv, sub=sub):
                                for half in range(2):
                                    for kc in range(KC):
                                        i = e.matmul(psV[pv][:, half * 128:(half + 1) * 128], lhsT=uT[:, kc, sub * 128:(sub + 1) * 128],
                                                     rhs=wA[sls[half]][:, kc, :], start=(kc == 0), stop=(kc == KC - 1))
                                return i
                            P.op("pe", mmv, reads=[("wA", sls[0]), ("wA", sls[1]), "uT"], writes=[("psV", pv)])
                            P.op("dve", lambda e, pv=pv: e.tensor_copy(out=vstg[pv][:], in_=psV[pv][:, 0:256]),
                                 reads=[("psV", pv)], writes=[("vstg", pv)])
                            tok0 = (t % 4) * TT + sub * 128
                            P.dma("sp", lambda e, pv=pv, vp=vp, tok0=tok0, t=t: e.dma_start(out=S1vs[t // 4][vp, tok0:tok0 + 128, :], in_=vstg[pv][:]),
                                  vstg_s[pv], reads=[("vstg", pv)], writes=["S1"])
                P.flush()

        PI = math.pi

        def tt(out, in0, in1, op, reads, writes):
            P.op("dve", lambda e: e.tensor_tensor(out=out, in0=in0, in1=in1, op=op), reads=reads, writes=writes)

        def localize(src_v, dst_l, sem=None):
            hold = {}
            load_dyn("sp", hold, "b4", 0, 4)
            load_dyn("sp", hold, "r", 1, 3)
            for rp in range(4):
                if len(src_v.shape) == 5:
                    P.dma("sp", lambda e, rp=rp: e.dma_start(
                        out=dst_l[rp:rp + 1], in_=src_v[rp:rp + 5][bass.ds(hold["b4"], 1), :, bass.ds(hold["r"], 1), :, :]),
                        dsem("loc"), writes=[("loc", rp)])
                else:
                    P.dma("sp", lambda e, rp=rp: e.dma_start(
                        out=dst_l[rp:rp + 1], in_=src_v[rp:rp + 5][bass.ds(hold["b4"], 1), :, bass.ds(hold["r"], 1), :]),
                        dsem("loc"), writes=[("loc", rp)])

        def load_dyn(eng_name, holder, key, col, maxv):
            def ld(e):
                reg = e.alloc_register(f"dyn_{key}_{col}_{ds_cnt[0]}")
                ds_cnt[0] += 1
                i = e.reg_load(reg, meta[0:1, col:col + 1])
                holder[key] = e.snap(reg, min_val=0, max_val=maxv)
                return i
            P.op(eng_name, ld)

        def ssm_stage(blk):
            tag = f"ss{blk}"
            with ExitStack() as es:
                def sb(name, shape, dt):
                    return es.enter_context(nc.sbuf_tensor(f"{tag}_{name}", list(shape), dt))

                sT = sb("sT", [128, 2, SEQ], BF16)
                sT_s = dsem("sT")
                Ur = sb("Ur", [128, 8, 512], F32)
                Ui = sb("Ui", [128, 8, 512], F32)
                Rt = sb("Rt", [128, 8, 512], F32)
                BBw = sb("BBw", [128, 2, 4, 2, 2, 64], BF16)
                CwPad = sb("CwPad", [128, 8, 2, 128], BF16)
                cols = sb("cols", [128, 8, 3], F32)
                rep = sb("rep", [128, 2, 3, 64], F32)
                bT = sb("bT", [128, 2, 2, 64], F32)
                cT = sb("cT", [128, 8, 2, 16], F32)
                dcol = sb("dcol", [128, 2], F32)
                mg8 = sb("mg8", [128, 8], F32)
                mgi = sb("mgi", [128, 2], F32)
                s_in = dsem("ssin")
                for dst_t, src, k in ((cols, ssm_cols_in, "cols"), (rep, ssm_rep_in, "rep"), (bT, bT_in, "bT"), (cT, cT_in, "cT"),
                                      (dcol, dcol_in, "dcol"), (mg8, mask_g8_in, "mg8"), (mgi, mask_gi_in, "mgi")):
                    P.dma("sp", lambda e, dst_t=dst_t, src=src: e.dma_start(out=dst_t[:], in_=(src[blk] if len(src.shape) == len(dst_t.shape) + 1 else src)), dsem("ssin_" + k), writes=[k])
                for rp in range(4):
                    for kc in range(2):
                        P.dma("sp", lambda e, rp=rp, kc=kc: e.dma_start(
                            out=sT[:, kc, rp * 2048:(rp + 1) * 2048], in_=LF5[rp, 2, blk, kc * 128:(kc + 1) * 128, :]),
                            sT_s, reads=[("loc", rp)], writes=["sT"])

                def prep_lambda(pfx, a_re, a_im, ldt, shape):
                    T = {}
                    for nm in ("dt", "adt", "th", "rho", "k", "tmp", "sin", "cos", "thc"):
                        T[nm] = sb(f"{pfx}_{nm}", shape, F32)
                    rd = [pfx + "_in"]
                    P.op("act", lambda e: e.activation(out=T["dt"][:], in_=ldt, func=AF.Exp), reads=rd, writes=[pfx + "dt"])
                    P.op("dve", lambda e: e.tensor_tensor(out=T["adt"][:], in0=a_re, in1=T["dt"][:], op=ALU.mult), reads=rd + [pfx + "dt"], writes=[pfx + "adt"])
                    P.op("dve", lambda e: e.tensor_tensor(out=T["th"][:], in0=a_im, in1=T["dt"][:], op=ALU.mult), reads=rd + [pfx + "dt"], writes=[pfx + "th"])
                    P.op("act", lambda e: e.activation(out=T["rho"][:], in_=T["adt"][:], func=AF.Exp), reads=[pfx + "adt"], writes=[pfx + "rho"])
                    for kk in range(4):
                        thr = (2 * kk + 1) * PI
                        P.op("dve", lambda e, thr=thr: e.tensor_scalar(out=T["tmp"][:], in0=T["th"][:], scalar1=thr, scalar2=-2.0 * PI,
                                                                       op0=ALU.is_ge, op1=ALU.mult), reads=[pfx + "th"], writes=[pfx + "tmp"])
                        dstk = "k"
                        if kk == 0:
                            P.op("dve", lambda e: e.tensor_copy(out=T["k"][:], in_=T["tmp"][:]), reads=[pfx + "tmp"], writes=[pfx + "k"])
                        else:
                            P.op("dve", lambda e: e.tensor_tensor(out=T["k"][:], in0=T["k"][:], in1=T["tmp"][:], op=ALU.add),
                                 reads=[pfx + "tmp", pfx + "k"], writes=[pfx + "k"])
                    P.op("dve", lambda e: e.tensor_tensor(out=T["th"][:], in0=T["th"][:], in1=T["k"][:], op=ALU.add),
                         reads=[pfx + "th", pfx + "k"], writes=[pfx + "th"])
                    P.op("act", lambda e: e.activation(out=T["sin"][:], in_=T["th"][:], func=AF.Sin), reads=[pfx + "th"], writes=[pfx + "sin"])
                    P.op("dve", lambda e: e.tensor_scalar(out=T["tmp"][:], in0=T["th"][:], scalar1=0.5 * PI, scalar2=-2.0 * PI,
                                                          op0=ALU.is_ge, op1=ALU.mult), reads=[pfx + "th"], writes=[pfx + "tmp"])
                    P.op("dve", lambda e: e.scalar_tensor_tensor(out=T["thc"][:], in0=T["th"][:], scalar=0.5 * PI, in1=T["tmp"][:],
                                                                 op0=ALU.add, op1=ALU.add), reads=[pfx + "th", pfx + "tmp"], writes=[pfx + "thc"])
                    P.op("act", lambda e: e.activation(out=T["cos"][:], in_=T["thc"][:], func=AF.Sin), reads=[pfx + "thc"], writes=[pfx + "cos"])
                    return T

                P.op("dve", lambda e: e.tensor_copy(out=cols[:], in_=cols[:]), reads=["cols"], writes=["c_in"])
                Tc = prep_lambda("c", cols[:, :, 0], cols[:, :, 1], cols[:, :, 2], [128, 8])
                P.op("dve", lambda e: e.tensor_copy(out=rep[:], in_=rep[:]), reads=["rep"], writes=["r_in"])
                Tr = prep_lambda("r", rep[:, :, 0, :], rep[:, :, 1, :], rep[:, :, 2, :], [128, 2, 64])
                W = {}
                for nm in ("lr", "li", "nr", "den", "rden", "t1", "t2", "cre", "cim", "bbr", "bbi"):
                    W[nm] = sb(f"w_{nm}", [128, 2, 64], F32)
                a_re_r, a_im_r = rep[:, :, 0, :], rep[:, :, 1, :]

                tt(W["lr"][:], Tr["rho"][:], Tr["cos"][:], ALU.mult, ["rrho", "rcos"], ["w_lr"])
                tt(W["li"][:], Tr["rho"][:], Tr["sin"][:], ALU.mult, ["rrho", "rsin"], ["w_li"])
                P.op("dve", lambda e: e.tensor_scalar(out=W["nr"][:], in0=W["lr"][:], scalar1=-1.0, scalar2=None, op0=ALU.add),
                     reads=["w_lr"], writes=["w_nr"])
                tt(W["t1"][:], a_re_r, a_re_r, ALU.mult, ["r_in"], ["w_t1"])
                tt(W["t2"][:], a_im_r, a_im_r, ALU.mult, ["r_in"], ["w_t2"])
                tt(W["den"][:], W["t1"][:], W["t2"][:], ALU.add, ["w_t1", "w_t2"], ["w_den"])
                P.op("dve", lambda e: e.reciprocal(out=W["rden"][:], in_=W["den"][:]), reads=["w_den"], writes=["w_rden"])
                tt(W["t1"][:], W["nr"][:], a_re_r, ALU.mult, ["w_nr", "r_in", "w_den"], ["w_t1"])
                tt(W["t2"][:], W["li"][:], a_im_r, ALU.mult, ["w_li", "r_in", "w_den"], ["w_t2"])
                tt(W["cre"][:], W["t1"][:], W["t2"][:], ALU.add, ["w_t1", "w_t2"], ["w_cre"])
                tt(W["cre"][:], W["cre"][:], W["rden"][:], ALU.mult, ["w_cre", "w_rden"], ["w_cre"])
                tt(W["t1"][:], W["li"][:], a_re_r, ALU.mult, ["w_li", "r_in", "w_cre"], ["w_t1"])
                tt(W["t2"][:], W["nr"][:], a_im_r, ALU.mult, ["w_nr", "r_in", "w_cre"], ["w_t2"])
                tt(W["cim"][:], W["t1"][:], W["t2"][:], ALU.subtract, ["w_t1", "w_t2"], ["w_cim"])
                tt(W["cim"][:], W["cim"][:], W["rden"][:], ALU.mult, ["w_cim", "w_rden"], ["w_cim"])
                b_r, b_i = bT[:, :, 0, :], bT[:, :, 1, :]
                tt(W["t1"][:], W["cre"][:], b_r, ALU.mult, ["w_cre", "bT", "w_cim"], ["w_t1"])
                tt(W["t2"][:], W["cim"][:], b_i, ALU.mult, ["w_cim", "bT"], ["w_t2"])
                tt(W["bbr"][:], W["t1"][:], W["t2"][:], ALU.subtract, ["w_t1", "w_t2"], ["w_bbr"])
                tt(W["t1"][:], W["cre"][:], b_i, ALU.mult, ["w_cre", "bT", "w_bbr"], ["w_t1"])
                tt(W["t2"][:], W["cim"][:], b_r, ALU.mult, ["w_cim", "bT", "w_bbr"], ["w_t2"])
                tt(W["bbi"][:], W["t1"][:], W["t2"][:], ALU.add, ["w_t1", "w_t2"], ["w_bbi"])
                for kc in range(2):
                    for ri, nm in ((0, "bbr"), (1, "bbi")):
                        for bpl in range(4):
                            P.op("dve", lambda e, kc=kc, ri=ri, nm=nm, bpl=bpl: e.tensor_tensor(
                                out=BBw[:, kc, bpl, ri, :, :],
                                in0=W[nm][:, kc, :].unsqueeze(1).broadcast_to([128, 2, 64]),
                                in1=mg8[:, 2 * bpl:2 * bpl + 2].unsqueeze(2).broadcast_to([128, 2, 64]), op=ALU.mult),
                                reads=["w_" + nm, "mg8"], writes=["BBw"])
                P.op("dve", lambda e: e.memset(CwPad[:], 0.0), writes=["CwPad"])
                for bp in range(8):
                    c0 = 32 * (bp % 4)
                    for ri in range(2):
                        P.op("dve", lambda e, bp=bp, ri=ri, c0=c0: e.tensor_tensor(
                            out=CwPad[:, bp, ri, c0:c0 + 32].rearrange("p (g h) -> p g h", g=2),
                            in0=cT[:, bp, ri, :].unsqueeze(1).broadcast_to([128, 2, 16]),
                            in1=mgi[:, :].unsqueeze(2).broadcast_to([128, 2, 16]), op=ALU.mult),
                            reads=["cT", "mgi", "CwPad"], writes=["CwPad"])
                P.op("dve", lambda e: e.tensor_scalar(out=CwPad[:, :, 1, :], in0=CwPad[:, :, 1, :], scalar1=-1.0, scalar2=None, op0=ALU.mult),
                     reads=["CwPad"], writes=["CwPad"])
                P.op("dve", lambda e: e.tensor_copy(out=Rt[:], in_=Tc["rho"][:].unsqueeze(2).broadcast_to([128, 8, 512])),
                     reads=["crho"], writes=["Rt"])
                wr = sb("wr", [128, 8], F32)
                wi = sb("wi", [128, 8], F32)
                wt1 = sb("wt1", [128, 8], F32)
                wt2 = sb("wt2", [128, 8], F32)
                u1 = sb("u1", [128, 8, 256], F32)
                u2 = sb("u2", [128, 8, 256], F32)
                P.op("dve", lambda e: e.tensor_copy(out=wr[:], in_=Tc["cos"][:]), reads=["ccos"], writes=["wr"])
                P.op("dve", lambda e: e.tensor_copy(out=wi[:], in_=Tc["sin"][:]), reads=["csin"], writes=["wi"])
                P.op("dve", lambda e: e.memset(Ur[:, :, 0:1], 1.0), writes=["Ur"])
                P.op("dve", lambda e: e.memset(Ui[:, :, 0:1], 0.0), writes=["Ui"])
                for k in range(9):
                    n = 1 << k
                    wrb = lambda n=n: wr[:].unsqueeze(2).broadcast_to([128, 8, n])
                    wib = lambda n=n: wi[:].unsqueeze(2).broadcast_to([128, 8, n])
                    tt(u1[:, :, 0:n], Ur[:, :, 0:n], wrb(), ALU.mult, ["Ur", "wr"], ["u1"])
                    tt(u2[:, :, 0:n], Ui[:, :, 0:n], wib(), ALU.mult, ["Ui", "wi"], ["u2"])
                    tt(Ur[:, :, n:2 * n], u1[:, :, 0:n], u2[:, :, 0:n], ALU.subtract, ["u1", "u2", "Ur"], ["Ur"])
                    tt(u1[:, :, 0:n], Ur[:, :, 0:n], wib(), ALU.mult, ["Ur", "wi"], ["u1"])
                    tt(u2[:, :, 0:n], Ui[:, :, 0:n], wrb(), ALU.mult, ["Ui", "wr"], ["u2"])
                    tt(Ui[:, :, n:2 * n], u1[:, :, 0:n], u2[:, :, 0:n], ALU.add, ["u1", "u2", "Ui"], ["Ui"])
                    tt(wt1[:], wr[:], wr[:], ALU.mult, ["wr"], ["wt1"])
                    tt(wt2[:], wi[:], wi[:], ALU.mult, ["wi"], ["wt2"])
                    tt(wi[:], wr[:], wi[:], ALU.mult, ["wr", "wi", "wt2"], ["wi"])
                    P.op("dve", lambda e: e.tensor_scalar(out=wi[:], in0=wi[:], scalar1=2.0, scalar2=None, op0=ALU.mult), reads=["wi"], writes=["wi"])
                    tt(wr[:], wt1[:], wt2[:], ALU.subtract, ["wt1", "wt2", "wr"], ["wr"])

                psX = es.enter_context(nc.psum_tensor(f"{tag}_psX", [128, 2, 2, 512], F32))
                psY = [es.enter_context(nc.psum_tensor(f"{tag}_psY{i}", [128, 512], F32)) for i in range(2)]
                A_ = sb("A", [128, 2, 512], F32)
                B_ = sb("B", [128, 2, 512], F32)
                t1 = sb("t1", [128, 2, 512], F32)
                t2 = sb("t2", [128, 2, 512], F32)
                G = sb("G", [128, 2, 2, 512], F32)
                H = [sb(f"H{i}", [128, 2, 2, 512], BF16) for i in range(2)]
                carry = sb("carry", [128, 8, 2], F32)
                ct1 = sb("ct1", [128, 2], F32)
                ct2 = sb("ct2", [128, 2], F32)
                yf = sb("yf", [128, 512], F32)
                yt = sb("yt", [128, 512], F32)
                ys = sb("ys", [128, 512], F32)
                yg = [sb(f"yg{i}", [128, 512], BF16) for i in range(2)]
                yg_s = [dsem("yg") for _ in range(2)]
                P.op("dve", lambda e: e.memset(carry[:], 0.0), writes=["carry"])
                yi = 0
                hi = 0
                for tb in range(16):
                    tok = slice(tb * 512, (tb + 1) * 512)
                    for bq in range(4):
                        kc = bq // 2
                        bps = (2 * bq, 2 * bq + 1)
                        hb = hi % 2
                        hi += 1

                        def mmx(e, kc=kc, bps=bps, tok=tok):
                            for i, bp in enumerate(bps):
                                for ri in range(2):
                                    ins = e.matmul(psX[:, i, ri, :], lhsT=BBw[:, kc, bp % 4, ri, :, :].rearrange("p g n -> p (g n)"),
                                                   rhs=sT[:, kc, tok], start=True, stop=True)
                            return ins
                        P.op("pe", mmx, reads=["BBw", "sT"], writes=["psX"])
                        Xr, Xi = psX[:, :, 0, :], psX[:, :, 1, :]
                        Urb, Uib = Ur[:, bps[0]:bps[0] + 2, :], Ui[:, bps[0]:bps[0] + 2, :]
                        tt(t1[:], Xr, Urb, ALU.mult, ["psX", "Ur"], ["t1"])
                        tt(t2[:], Xi, Uib, ALU.mult, ["psX", "Ui"], ["t2"])
                        tt(A_[:], t1[:], t2[:], ALU.add, ["t1", "t2"], ["A"])
                        tt(t1[:], Xi, Urb, ALU.mult, ["psX", "Ur", "A"], ["t1"])
                        tt(t2[:], Xr, Uib, ALU.mult, ["psX", "Ui", "A"], ["t2"])
                        tt(B_[:], t1[:], t2[:], ALU.subtract, ["t1", "t2"], ["B"])
                        for i, bp in enumerate(bps):
                            for ri, src in ((0, A_), (1, B_)):
                                P.op("dve", lambda e, i=i, bp=bp, ri=ri, src=src: e.tensor_tensor_scan(
                                    out=G[:, i, ri, :], data0=Rt[:, bp, :], data1=src[:, i, :], initial=carry[:, bp, ri:ri + 1],
                                    op0=ALU.mult, op1=ALU.add), reads=["Rt", "A", "B", "carry"], writes=["G"])
                        Ger, Gei = G[:, :, 0, 511], G[:, :, 1, 511]
                        wrs, wis = wr[:, bps[0]:bps[0] + 2], wi[:, bps[0]:bps[0] + 2]
                        tt(ct1[:], Ger, wrs, ALU.mult, ["G", "wr"], ["ct1"])
                        tt(ct2[:], Gei, wis, ALU.mult, ["G", "wi"], ["ct2"])
                        tt(carry[:, bps[0]:bps[0] + 2, 0], ct1[:], ct2[:], ALU.subtract, ["ct1", "ct2", "carry"], ["carry"])
                        tt(ct1[:], Ger, wis, ALU.mult, ["G", "wi", "carry"], ["ct1"])
                        tt(ct2[:], Gei, wrs, ALU.mult, ["G", "wr", "carry"], ["ct2"])
                        tt(carry[:, bps[0]:bps[0] + 2, 1], ct1[:], ct2[:], ALU.add, ["ct1", "ct2", "carry"], ["carry"])
                        Gr, Gi = G[:, :, 0, :], G[:, :, 1, :]
                        hk = ("H", hb)
                        tt(t1[:], Gr, Urb, ALU.mult, ["G", "Ur"], ["t1"])
                        tt(t2[:], Gi, Uib, ALU.mult, ["G", "Ui"], ["t2"])
                        tt(H[hb][:, :, 0, :], t1[:], t2[:], ALU.subtract, ["t1", "t2"], [hk])
                        tt(t1[:], Gr, Uib, ALU.mult, ["G", "Ui", hk], ["t1"])
                        tt(t2[:], Gi, Urb, ALU.mult, ["G", "Ur", hk], ["t2"])
                        tt(H[hb][:, :, 1, :], t1[:], t2[:], ALU.add, ["t1", "t2", hk], [hk])

                        def mmy(e, kc=kc, bps=bps, hb=hb, bq=bq):
                            for i, bp in enumerate(bps):
                                for ri in range(2):
                                    first = (bq % 2 == 0 and i == 0 and ri == 0)
                                    last = (bq % 2 == 1 and i == 1 and ri == 1)
                                    ins = e.matmul(psY[kc][:], lhsT=CwPad[:, bp, ri, :], rhs=H[hb][:, i, ri, :], start=first, stop=last)
                            return ins
                        P.op("pe", mmy, reads=["CwPad", hk], writes=[("psY", kc)])
                        if bq % 2 == 1:
                            yb = yi % 2
                            yi += 1
                            P.op("dve", lambda e, kc=kc, tok=tok: e.scalar_tensor_tensor(
                                out=yf[:], in0=sT[:, kc, tok], scalar=dcol[:, kc:kc + 1], in1=psY[kc][:], op0=ALU.mult, op1=ALU.add),
                                reads=["sT", "dcol", ("psY", kc)], writes=["yf"])
                            tt(yt[:], yf[:], yf[:], ALU.mult, ["yf"], ["yt"])
                            P.op("dve", lambda e: e.tensor_scalar(out=yt[:], in0=yt[:], scalar1=0.044715, scalar2=1.0, op0=ALU.mult, op1=ALU.add),
                                 reads=["yt"], writes=["yt"])
                            tt(yt[:], yt[:], yf[:], ALU.mult, ["yt", "yf"], ["yt"])
                            P.op("act", lambda e: e.activation(out=ys[:], in_=yt[:], func=AF.Sigmoid, scale=1.5957691216057308),
                                 reads=["yt"], writes=["ys"])
                            tt(yg[yb][:], yf[:], ys[:], ALU.mult, ["yf", "ys"], [("yg", yb)])
                            P.dma("sp", lambda e, yb=yb, kc=kc, tok=tok: e.dma_start(out=L2F[blk, 256 + kc * 128:256 + (kc + 1) * 128, tok], in_=yg[yb][:]),
                                  yg_s[yb], reads=[("yg", yb)], writes=["S2"])
                P.flush()

        def attn_stage(blk):
            tag = f"at{blk}"
            with ExitStack() as es:
                def sb(name, shape, dt):
                    return es.enter_context(nc.sbuf_tensor(f"{tag}_{name}", list(shape), dt))

                def ps(name):
                    return es.enter_context(nc.psum_tensor(f"{tag}_{name}", [128, 512], F32))

                qT = [sb(f"qT{h}", [128, SEQ], BF16) for h in range(2)]
                kT = [sb(f"kT{h}", [128, SEQ], BF16) for h in range(2)]
                vv = [sb(f"v{h}", [128, 64, 128], BF16) for h in range(2)]
                ld_s = {(t_, h_): dsem("atld") for t_ in "qkv" for h_ in range(2)}
                lam4 = sb("lam4", [64, 4], F32)
                lrep = sb("lrep", [64, 2, 128], F32)
                subg = sb("subg", [128, 1], F32)
                trif = sb("trif", [128, 128], F32)
                trib = sb("trib", [128, 128], BF16)
                onesb = sb("onesb", [128, 128], BF16)
                onesf = sb("onesf", [128, 128], F32)
                e12 = sb("e12", [128, 2], F32)
                neglam = sb("neglam", [128, 1], F32)
                s_c = dsem("atc")
                pT = [[sb(f"pT{c}{i}", [128, 512], BF16) for i in range(3)] for c in range(2)]
                rl0 = sb("rl0", [128, 512], F32)
                rl1 = sb("rl1", [128, 512], F32)
                o0 = sb("o0", [128, 512], F32)
                o1 = sb("o1", [128, 512], F32)
                sq = sb("sq", [128, 512], F32)
                rr = sb("rr", [128, 512], F32)
                ob = [sb(f"ob{i}", [128, 512], BF16) for i in range(2)]
                ob_s = [dsem("ob") for _ in range(2)]
                psS = [[ps(f"psS{c}{i}") for i in range(2)] for c in range(2)]
                psO = [ps(f"psO{c}") for c in range(2)]
                psL = [ps(f"psL{c}") for c in range(2)]

                P.dma("sp", lambda e: e.dma_start(out=lam4[:], in_=lam4_in), dsem("lam4"), writes=["lam4"])
                P.dma("sp", lambda e: e.dma_start(out=subg[:], in_=subg_in), dsem("subg"), writes=["subg"])
                P.dma("sp", lambda e: e.dma_start(out=trif[:], in_=tri_in), dsem("trif"), writes=["trif"])
                for hh in range(2):
                    for rp in range(4):
                        rows = slice(hh * 128, (hh + 1) * 128)
                        P.dma("sp", lambda e, hh=hh, rp=rp, rows=rows: e.dma_start(
                            out=qT[hh][:, rp * 2048:(rp + 1) * 2048], in_=LF5[rp, 0, blk, rows, :]), ld_s[("q", hh)], reads=[("loc", rp)], writes=[("qT", hh)])
                        P.dma("sp", lambda e, hh=hh, rp=rp, rows=rows: e.dma_start(
                            out=kT[hh][:, rp * 2048:(rp + 1) * 2048], in_=LF5[rp, 1, blk, rows, :]), ld_s[("k", hh)], reads=[("loc", rp)], writes=[("kT", hh)])
                for hh in range(2):
                    for rp in range(4):
                        rows = slice(hh * 128, (hh + 1) * 128)
                        P.dma("sp", lambda e, hh=hh, rp=rp, rows=rows: e.dma_start(
                            out=vv[hh][:, rp * 16:(rp + 1) * 16, :],
                            in_=LF5[rp, 3, blk, :, :].rearrange("(kt a) (b e) -> (a b) kt e", a=16, b=8, e=256)[:, :, hh * 128:(hh + 1) * 128]),
                            ld_s[("v", hh)], writes=[("v", hh)])
                P.op("dve", lambda e: e.tensor_copy(out=trib[:], in_=trif[:]), reads=["trif"], writes=["trib"])
                P.op("dve", lambda e: e.memset(onesb[:], 1.0), writes=["onesb"])
                P.op("dve", lambda e: e.memset(onesf[:], 1.0), writes=["onesf"])
                P.op("dve", lambda e: e.tensor_copy(out=lrep[:, 0, :], in_=lam4[:, 0:1].broadcast_to([64, 128])), reads=["lam4"], writes=["lrep"])
                P.op("dve", lambda e: e.tensor_copy(out=lrep[:, 1, :], in_=lam4[:, 2:3].broadcast_to([64, 128])), reads=["lam4", "lrep"], writes=["lrep"])

                def mml(e):
                    e.matmul(psL[0][:, 0:1], lhsT=lrep[:, 0, :], rhs=lam4[:, 1:2], start=True, stop=True)
                    return e.matmul(psL[0][:, 1:2], lhsT=lrep[:, 1, :], rhs=lam4[:, 3:4], start=True, stop=True)
                P.op("pe", mml, reads=["lrep", "lam4"], writes=[("psL", 0)])
                P.op("act", lambda e: e.activation(out=e12[:], in_=psL[0][:, 0:2], func=AF.Exp), reads=[("psL", 0)], writes=["e12"])
                P.op("dve", lambda e: e.scalar_tensor_tensor(out=neglam[:], in0=e12[:, 1:2], scalar=-LAMBDA_INIT, in1=e12[:, 0:1],
                                                             op0=ALU.add, op1=ALU.subtract), reads=["e12"], writes=["neglam"])
                P.op("dve", lambda e: e.tensor_scalar(out=subg[:], in0=subg[:], scalar1=1.0 - LAMBDA_INIT, scalar2=None, op0=ALU.mult),
                     reads=["subg"], writes=["subg"])

                SCALE = 1.0 / math.sqrt(64.0)
                for hh in range(2):
                    steps = []
                    for qb in range(16):
                        nk = 4 * qb + 4
                        for kt in range(nk):
                            steps.append((qb, kt, kt == nk - 1))
                    n = len(steps)
                    oi = [0]

                    def emit_qk(idx):
                        qb, kt, _ = steps[idx]
                        col0 = 128 * (kt - 4 * qb) if kt >= 4 * qb else 0
                        par = idx % 2
                        pp = idx % 3

                        def qk(e, kt=kt, qb=qb, col0=col0, par=par, hh=hh):
                            for c in range(2):
                                i = e.matmul(psS[c][par][:, col0:512], lhsT=kT[hh][64 * c:64 * c + 64, kt * 128:(kt + 1) * 128],
                                             rhs=qT[hh][64 * c:64 * c + 64, qb * 512 + col0:(qb + 1) * 512], start=True, stop=True)
                            return i
                        P.op("pe", qk, reads=[("qT", hh), ("kT", hh)], writes=[("psS", 0, par), ("psS", 1, par)])
                        for c in range(2):
                            P.op("act", lambda e, c=c, par=par, pp=pp, col0=col0: e.activation(
                                out=pT[c][pp][:, col0:512], in_=psS[c][par][:, col0:512], func=AF.Exp, scale=SCALE),
                                reads=[("psS", c, par)], writes=[("pT", c, pp)])
                            if kt >= 4 * qb:
                                P.op("dve", lambda e, c=c, pp=pp, col0=col0: e.tensor_tensor(
                                    out=pT[c][pp][:, col0:col0 + 128], in0=pT[c][pp][:, col0:col0 + 128], in1=trib[:], op=ALU.mult),
                                    reads=[("pT", c, pp), "trib"], writes=[("pT", c, pp)])

                    def emit_pv(idx):
                        qb, kt, last = steps[idx]
                        col0 = 128 * (kt - 4 * qb) if kt >= 4 * qb else 0
                        pp = idx % 3

                        def pv(e, kt=kt, col0=col0, pp=pp, last=last, hh=hh):
                            for c in range(2):
                                e.matmul(psO[c][:, col0:512], lhsT=vv[hh][:, kt, :], rhs=pT[c][pp][:, col0:512], start=(kt == 0), stop=last)
                                i = e.matmul(psL[c][:, col0:512], lhsT=onesb[:], rhs=pT[c][pp][:, col0:512], start=(kt == 0), stop=last)
                            return i
                        P.op("pe", pv, reads=[("v", hh), "onesb", ("pT", 0, pp), ("pT", 1, pp)],
                             writes=[("psO", 0), ("psO", 1), ("psL", 0), ("psL", 1)])
                        if last:
                            b_ = oi[0] % 2
                            oi[0] += 1
                            P.op("dve", lambda e: e.reciprocal(out=rl0[:], in_=psL[0][:]), reads=[("psL", 0)], writes=["rl0"])
                            P.op("dve", lambda e: e.reciprocal(out=rl1[:], in_=psL[1][:]), reads=[("psL", 1)], writes=["rl1"])
                            tt(o0[:], psO[0][:], rl0[:], ALU.mult, [("psO", 0), "rl0"], ["o0"])
                            tt(o1[:], psO[1][:], rl1[:], ALU.mult, [("psO", 1), "rl1"], ["o1"])
                            P.op("dve", lambda e: e.scalar_tensor_tensor(out=o0[:], in0=o1[:], scalar=neglam[:, 0:1], in1=o0[:],
                                                                         op0=ALU.mult, op1=ALU.add), reads=["o0", "o1", "neglam"], writes=["o0"])
                            tt(sq[:], o0[:], o0[:], ALU.mult, ["o0"], ["sq"])
                            P.op("pe", lambda e: e.matmul(psL[0][:], lhsT=onesf[:], rhs=sq[:], start=True, stop=True),
                                 reads=["onesf", "sq", ("psL", 0)], writes=[("psL", 0)])
                            P.op("dve", lambda e: e.tensor_scalar(out=rr[:], in0=psL[0][:], scalar1=1.0 / 128.0, scalar2=RMS_EPS,
                                                                  op0=ALU.mult, op1=ALU.add), reads=[("psL", 0)], writes=["rr"])
                            P.op("act", lambda e: e.sqrt(out=rr[:], in_=rr[:]), reads=["rr"], writes=["rr"])
                            P.op("dve", lambda e: e.reciprocal(out=rr[:], in_=rr[:]), reads=["rr"], writes=["rr"])
                            tt(o0[:], o0[:], rr[:], ALU.mult, ["o0", "rr"], ["o0"])
                            P.op("dve", lambda e, b_=b_: e.tensor_scalar(out=ob[b_][:], in0=o0[:], scalar1=subg[:, 0:1], scalar2=None, op0=ALU.mult),
                                 reads=["o0", "subg"], writes=[("ob", b_)])
                            P.dma("sp", lambda e, b_=b_, qb=qb, hh=hh: e.dma_start(out=L2F[blk, hh * 128:(hh + 1) * 128, qb * 512:(qb + 1) * 512], in_=ob[b_][:]),
                                  ob_s[b_], reads=[("ob", b_)], writes=["S2"])

                    for idx in range(n + 1):
                        if idx < n:
                            emit_qk(idx)
                        if idx >= 1:
                            emit_pv(idx - 1)
                P.flush()

        def mixout_stage():
            tag = "mo"
            s = 1
            with ExitStack() as es:
                def sb(name, shape, dt):
                    return es.enter_context(nc.sbuf_tensor(f"{tag}_{name}", list(shape), dt))

                def ps(name):
                    return es.enter_context(nc.psum_tensor(f"{tag}_{name}", [128, 512], F32))

                wout = sb("wout", [128, KC, D], BF16)
                gluw = sb("gluw", [128, 8, 1024], BF16)
                glub = sb("glub", [128, 8], F32)
                mixT = sb("mixT", [128, 16, TT], BF16)
                gT8 = sb("gT8", [128, 8, TT], BF16)
                gate = [sb(f"gate{i}", [128, TT], F32) for i in range(2)]
                pre = sb("pre", [128, 4, D], F32)
                xs = [sb(f"xs{i}", [128, D], F32) for i in range(2)]
                xs_s = [dsem("mxs") for _ in range(2)]
                gvec = sb("gvec", [128, D], F32)
                lng = sb("lng", [128, D], F32)
                lnb = sb("lnb", [128, D], F32)
                stats = sb("stats", [128, 4, 6], F32)
                mv = sb("mv", [128, 2], F32)
                rstd = sb("rstd", [128, 1], F32)
                nmr = sb("nmr", [128, 1], F32)
                psG = [ps(f"psG{i}") for i in range(2)]
                psO = [ps(f"psO{i}") for i in range(2)]
                s_st = [dsem("mst") for _ in range(4)]
                s_ma = dsem("mixA")
                s_g8 = dsem("gT8")

                wov = w_out.rearrange("(kc p) d -> p kc d", p=128)
                for j in range(4):
                    P.dma("pool", lambda e, j=j: e.dma_start(out=wout[:, :, j * 512:(j + 1) * 512], in_=wov[:, :, j * 512:(j + 1) * 512]),
                          dsem("wout", "pool"), writes=[("wout", j)])
                P.dma("pool", lambda e: e.dma_start(out=gluw[:], in_=glu_w.rearrange("(kc p) c -> p kc c", p=128)), dsem("gluw", "pool"), writes=["gluw"])
                P.dma("sp", lambda e: e.dma_start(out=glub[:], in_=glu_bT), dsem("glub"), writes=["glub"])
                P.dma("sp", lambda e: e.dma_start(out=gvec[:], in_=gvec_d[s]), dsem("gvec"), writes=["gvec"])
                P.dma("sp", lambda e: e.dma_start(out=lng[:], in_=ln_g[s:s + 1, :].partition_broadcast(128)), dsem("lng"), writes=["lng"])
                P.dma("sp", lambda e: e.dma_start(out=lnb[:], in_=ln_b[s:s + 1, :].partition_broadcast(128)), dsem("lnb"), writes=["lnb"])
                xi = 0
                gi_ = 0
                oi = 0
                for t in range(16):
                    cols = slice(t * TT, (t + 1) * TT)
                    for rp in range(4):
                        for hh in range(2):
                            P.dma("sp", lambda e, rp=rp, hh=hh, cols=cols: e.dma_start(
                                out=mixT[:, rp * 2 + hh, :], in_=L2F[rp, hh * 128:(hh + 1) * 128, cols]),
                                s_ma, reads=[("loc", rp)], writes=["mixA"])
                            P.dma("sp", lambda e, rp=rp, hh=hh, cols=cols: e.dma_start(
                                out=gT8[:, rp * 2 + hh, :], in_=L2F[rp, 256 + hh * 128:256 + (hh + 1) * 128, cols]),
                                s_g8, reads=[("loc", rp)], writes=["gT8"])
                    for cj in range(8):
                        par = gi_ % 2
                        gi_ += 1

                        def mmg(e, cj=cj, par=par):
                            for kc in range(8):
                                i = e.matmul(psG[par][:], lhsT=gluw[:, kc, cj * 128:(cj + 1) * 128], rhs=gT8[:, kc, :], start=(kc == 0), stop=(kc == 7))
                            return i
                        P.op("pe", mmg, reads=["gluw", "gT8"], writes=[("psG", par)])
                        P.op("act", lambda e, cj=cj, par=par: e.activation(out=gate[par][:], in_=psG[par][:], func=AF.Sigmoid, bias=glub[:, cj:cj + 1]),
                             reads=[("psG", par), "glub"], writes=[("gate", par)])
                        P.op("dve", lambda e, cj=cj, par=par: e.tensor_tensor(out=mixT[:, 8 + cj, :], in0=gT8[:, cj, :], in1=gate[par][:], op=ALU.mult),
                             reads=["gT8", ("gate", par)], writes=["mixS"])
                    for sub in range(4):
                        for dblk in range(4):
                            po = oi % 2
                            oi += 1

                            def mmo(e, sub=sub, dblk=dblk, po=po):
                                for kc in range(KC):
                                    i = e.matmul(psO[po][:], lhsT=mixT[:, kc, sub * 128:(sub + 1) * 128], rhs=wout[:, kc, dblk * 512:(dblk + 1) * 512],
                                                 start=(kc == 0), stop=(kc == KC - 1))
                                return i
                            P.op("pe", mmo, reads=["mixA", "mixS", ("wout", dblk)], writes=[("psO", po)])
                            P.op("dve", lambda e, sub=sub, dblk=dblk, po=po: e.tensor_tensor(
                                out=pre[:, sub, dblk * 512:(dblk + 1) * 512], in0=psO[po][:], in1=gvec[:, dblk * 512:(dblk + 1) * 512], op=ALU.mult),
                                reads=[("psO", po), "gvec"], writes=[("pre", sub)])
                    for sub in range(4):
                        r0 = t * TT + sub * 128
                        xb = xi % 2
                        xi += 1
                        P.dma("sp", lambda e, xb=xb, r0=r0: e.dma_start(out=xs[xb][:], in_=x1_full[r0:r0 + 128, :]),
                              xs_s[xb], writes=[("xs", xb)])
                        pk = ("pre", sub)
                        P.op("dve", lambda e, xb=xb, sub=sub: e.scalar_tensor_tensor(
                            out=pre[:, sub, :], in0=xs[xb][:], scalar=ALPHA, in1=pre[:, sub, :], op0=ALU.mult, op1=ALU.add),
                            reads=[("xs", xb), pk], writes=[pk])

                        def bst(e, sub=sub):
                            for a in range(4):
                                i = e.bn_stats(out=stats[:, a, :], in_=pre[:, sub, a * 512:(a + 1) * 512])
                            return i
                        P.op("dve", bst, reads=[pk], writes=["stats"])
                        P.op("dve", lambda e: e.bn_aggr(out=mv[:], in_=stats[:].rearrange("p a b -> p (a b)")), reads=["stats"], writes=["mv"])
                        P.op("dve", lambda e: e.tensor_scalar(out=rstd[:], in0=mv[:, 1:2], scalar1=LN_EPS, scalar2=None, op0=ALU.add),
                             reads=["mv"], writes=["rstd"])
                        P.op("act", lambda e: e.sqrt(out=rstd[:], in_=rstd[:]), reads=["rstd"], writes=["rstd"])
                        P.op("dve", lambda e: e.reciprocal(out=rstd[:], in_=rstd[:]), reads=["rstd"], writes=["rstd"])
                        P.op("dve", lambda e: e.scalar_tensor_tensor(out=nmr[:], in0=mv[:, 0:1], scalar=-1.0, in1=rstd[:],
                                                                     op0=ALU.mult, op1=ALU.mult), reads=["mv", "rstd"], writes=["nmr"])
                        P.op("dve", lambda e, sub=sub: e.tensor_scalar(out=pre[:, sub, :], in0=pre[:, sub, :], scalar1=rstd[:], scalar2=nmr[:],
                                                                       op0=ALU.mult, op1=ALU.add), reads=[pk, "rstd", "nmr"], writes=[pk])
                        P.op("dve", lambda e, sub=sub: e.tensor_tensor(out=pre[:, sub, :], in0=pre[:, sub, :], in1=lng[:], op=ALU.mult),
                             reads=[pk, "lng"], writes=[pk])
                        P.op("dve", lambda e, sub=sub: e.tensor_tensor(out=pre[:, sub, :], in0=pre[:, sub, :], in1=lnb[:], op=ALU.add),
                             reads=[pk, "lnb"], writes=[pk])
                        P.dma("sp", lambda e, sub=sub, r0=r0: e.dma_start(out=x2_d[r0:r0 + 128, :], in_=pre[:, sub, :]), s_st[sub],
                              reads=[pk], writes=[("mo_dst", r0)])
                P.flush()

        ffn_stage(0, x_in, x1_full, *ffw[1], tag="f1", ntt=16)
        proj_stage()
        for blk in range(4):
            ssm_stage(blk)
        for blk in range(4):
            attn_stage(blk)
        if stop is not None and stop.startswith("l2f"):
            bk = int(stop[3])
            sd = DSem(nc, "dbgsem")
            with nc.Block() as blk_:
                @blk_.gpsimd
                def _(g):
                    g.dma_start(out=out_d.rearrange("(a b) c -> a (b c)", a=512), in_=L2F[bk]).then_inc(sd.sem, 16)
                    g.wait_ge(sd.sem, 16)
            return nc, P
        mixout_stage()
        if stop == "mixout":
            sd = DSem(nc, "dbgsem")
            with nc.Block() as blk_:
                @blk_.gpsimd
                def _(g):
                    g.dma_start(out=out_d, in_=x2_d).then_inc(sd.sem, 16)
                    g.wait_ge(sd.sem, 16)
            return nc, P
        ffn_stage(2, x2_d, out_d, *ffw[2], tag="f2", ntt=16)
    return nc, P


def make_in_maps(inputs):
    x = np.asarray(inputs["x"], dtype=np.float32)
    c = np.asarray(inputs["c"], dtype=np.float32)
    ident = np.eye(128, dtype=np.float32)
    maps = []
    for core in range(8):
        b, r = core // 4, core % 4
        m = {
            "x": np.ascontiguousarray(x[b]),
            "c_col": np.ascontiguousarray(c[b].reshape(KC, 128).T),
            "w_cond": np.ascontiguousarray(inputs["w_cond"][0], dtype=np.float32),
            "b_cond": np.ascontiguousarray(inputs["b_cond"], dtype=np.float32),
            "ln_g": np.ascontiguousarray(inputs["ln_g"][0], dtype=np.float32),
            "ln_b": np.ascontiguousarray(inputs["ln_b"][0], dtype=np.float32),
            "ident": ident,
            "w_in": np.ascontiguousarray(inputs["w_in"][0], dtype=np.float32),
            "meta": np.array([[4 * b, r, 0, ((core + 3) % 8) * 512, 0, 0, 0, 0]], dtype=np.int32),
        }
        f32 = lambda a: np.ascontiguousarray(a, dtype=np.float32)
        colsL, repL, bTL, cTL, dL = [], [], [], [], []
        for blk_ in range(4):
            g0 = 16 * blk_
            a_re, a_im, ldt = inputs["ssm_a_re"][0], inputs["ssm_a_im"][0], inputs["ssm_log_dt"][0]
            b_re, b_im = inputs["ssm_b_re"][0], inputs["ssm_b_im"][0]
            c_re, c_im = inputs["ssm_c_re"][0], inputs["ssm_c_im"][0]
            cols = np.zeros((128, 8, 3), np.float32)
            cT = np.zeros((128, 8, 2, 16), np.float32)
            for bp in range(8):
                for gi in range(2):
                    g = g0 + 2 * bp + gi
                    cols[gi * 64:(gi + 1) * 64, bp, 0] = a_re[g]
                    cols[gi * 64:(gi + 1) * 64, bp, 1] = a_im[g]
                    cols[gi * 64:(gi + 1) * 64, bp, 2] = ldt[g]
                    cT[gi * 64:(gi + 1) * 64, bp, 0, :] = c_re[g].T
                    cT[gi * 64:(gi + 1) * 64, bp, 1, :] = c_im[g].T
            rep = np.zeros((128, 2, 3, 64), np.float32)
            bT = np.zeros((128, 2, 2, 64), np.float32)
            for kc in range(2):
                for gl in range(8):
                    g = g0 + 8 * kc + gl
                    rep[gl * 16:(gl + 1) * 16, kc, 0, :] = a_re[g][None, :]
                    rep[gl * 16:(gl + 1) * 16, kc, 1, :] = a_im[g][None, :]
                    rep[gl * 16:(gl + 1) * 16, kc, 2, :] = ldt[g]
                    bT[gl * 16:(gl + 1) * 16, kc, 0, :] = b_re[g].T
                    bT[gl * 16:(gl + 1) * 16, kc, 1, :] = b_im[g].T
            colsL.append(cols)
            repL.append(rep)
            bTL.append(bT)
            cTL.append(cT)
            dL.append(f32(inputs["ssm_d"][0][g0 * 16:g0 * 16 + 256].reshape(2, 128).T))
        m["ssm_cols"] = np.stack(colsL)
        m["ssm_rep"] = np.stack(repL)
        m["bT"] = np.stack(bTL)
        m["cT"] = np.stack(cTL)
        m["dcol"] = np.stack(dL)
        pidx = np.arange(128)
        m["mask_g8"] = f32((pidx[:, None] // 16) == np.arange(8)[None, :])
        m["mask_gi"] = f32((pidx[:, None] // 64) == np.arange(2)[None, :])
        m["lam4"] = f32(np.stack([inputs["lambda_q1"][0], inputs["lambda_k1"][0], inputs["lambda_q2"][0], inputs["lambda_k2"][0]], axis=1))
        m["subg"] = f32(inputs["subln_g"][0].reshape(128, 1))
        m["tri"] = f32(pidx[:, None] <= pidx[None, :])
        m["glu_w"] = f32(inputs["glu_w"][0])
        m["glu_bT"] = f32(inputs["glu_b"][0].reshape(8, 128).T)
        m["w_out"] = f32(inputs["w_out"][0])
        for s in (1, 2):
            for w in ("w1", "w3", "w2"):
                m[f"ffn{s}_{w}"] = np.ascontiguousarray(inputs[f"ffn{s}_{w}"][0], dtype=np.float32)
        maps.append(m)
    return maps


def kernel(**inputs):
    nc, _ = build_program(DEBUG_STOP)
    maps = make_in_maps(inputs)
    res = run_bass_kernel_spmd(nc, maps, core_ids=list(range(8)))
    out = np.zeros((2, SEQ, D), dtype=np.float32)
    for b in range(2):
        out[b] = res.results[4 * b]["out"]
    return out
```
